# Optimizing a Trainium2 kernel written in Bass

```python
import math
import jax
import jax.numpy as jnp
from jax import lax
import numpy as np

D_MODEL = 2048
BATCH = 2
SEQ = 16384
DEPTH = 4

GRID_W = 64
CTX_LEN = 256
N_MIXERS = 3
HEAD_DIM = 128
N_HEADS = D_MODEL // HEAD_DIM
N_KV_HEADS = 4
GQA_GROUP = N_HEADS // N_KV_HEADS
Q_BLOCK = 128
ROPE_THETA = 10000.0
D_FF = -(-8 * D_MODEL // (3 * 256)) * 256
HYENA_EMB = 33
HYENA_ORDER = 64
HYENA_FAST_DECAY = 0.3
HYENA_SLOW_DECAY = 1.5
HYENA_TARGET = 1e-2
N_MOD = 6
NORM_EPS = 1e-6
F32 = jnp.float32

kernel_name = 'hybrid_hyena_shortconv_gqa_dit_trunk'


def rmsnorm(x, g):
    xf = x.astype(F32)
    y = xf * lax.rsqrt(jnp.mean(xf * xf, axis=-1, keepdims=True) + NORM_EPS)
    return (y * g.astype(F32)).astype(x.dtype)


def modulate(h, shift, scale):
    return h * (1 + scale) + shift


def adaln(cond, w_mod, b_mod):
    m = jax.nn.silu(cond) @ w_mod + b_mod
    return jnp.split(m[..., None, :], N_MOD, axis=-1)


def dwconv3(u, w):
    up = jnp.pad(u, ((0, 0), (1, 1), (0, 0)))
    return up[:, :-2] * w[0] + up[:, 1:-1] * w[1] + up[:, 2:] * w[2]


def swiglu(h, w_gate, w_up, w_down):
    return (jax.nn.silu(h @ w_gate) * (h @ w_up)) @ w_down


def hyena_filters(L, w1, b1, w2, b2, w3, b3, w4, freq):
    t = jnp.linspace(0.0, 1.0, L, dtype=F32)[:, None]
    bands = (HYENA_EMB - 1) // 2
    omega = 2.0 * math.pi * jnp.arange(L, dtype=F32)[:, None] / L
    fb = jnp.linspace(1e-4, bands - 1, bands, dtype=F32)[None, :]
    z = jnp.concatenate([t, jnp.cos(fb * omega), -jnp.sin(fb * omega)], axis=-1)
    fr = freq.astype(F32)
    hdn = jnp.sin(fr * (z @ w1.astype(F32) + b1.astype(F32)))
    hdn = jnp.sin(fr * (hdn @ w2.astype(F32) + b2.astype(F32)))
    hdn = jnp.sin(fr * (hdn @ w3.astype(F32) + b3.astype(F32)))
    h = hdn @ w4.astype(F32)
    max_decay = math.log(HYENA_TARGET) / HYENA_FAST_DECAY
    min_decay = math.log(HYENA_TARGET) / HYENA_SLOW_DECAY
    deltas = jnp.linspace(min_decay, max_decay, D_MODEL, dtype=F32)
    decay = jnp.exp(-t * jnp.abs(deltas)[None, :])
    h = h.reshape(L, 2, D_MODEL) * decay[:, None, :]
    return h[:, 0], h[:, 1]


def bidir_longconv(u, k_fwd, k_bwd):
    _, L, C = u.shape
    n = 2 * L
    k = jnp.concatenate([k_fwd, jnp.zeros((1, C), F32), k_bwd[:0:-1]], axis=0)
    U = jnp.fft.rfft(u.astype(F32), n=n, axis=1)
    K = jnp.fft.rfft(k, n=n, axis=0)
    y = jnp.fft.irfft(U * K[None], n=n, axis=1)[:, :L]
    return y.astype(u.dtype)


def hyena_mixer(h, w_in, b_in, short_w, short_b, f_w1, f_b1, f_w2, f_b2, f_w3, f_b3,
                f_w4, f_freq, skip_d, w_out, b_out):
    L = h.shape[1]
    u = dwconv3(h @ w_in + b_in, short_w) + short_b
    x0, x1, v = jnp.split(u, 3, axis=-1)
    k_fwd, k_bwd = hyena_filters(L, f_w1, f_b1, f_w2, f_b2, f_w3, f_b3, f_w4, f_freq)
    v = v * x1
    v = bidir_longconv(v, k_fwd, k_bwd) + v * skip_d
    return (v * x0) @ w_out + b_out


def shortconv_mixer(h, w_in, conv_w, w_out):
    gb, gc, xv = jnp.split(h @ w_in, 3, axis=-1)
    return (gb * dwconv3(gc * xv, conv_w)) @ w_out


def axial_rope_angles(L):
    rows = L // GRID_W
    d_axis = HEAD_DIM // 2
    inv_freq = ROPE_THETA ** (-jnp.arange(0, d_axis, 2, dtype=F32) / d_axis)
    ang_r = jnp.arange(rows, dtype=F32)[:, None] * inv_freq
    ang_c = jnp.arange(GRID_W, dtype=F32)[:, None] * inv_freq
    half = d_axis // 2
    ang = jnp.concatenate([jnp.broadcast_to(ang_r[:, None, :], (rows, GRID_W, half)),
                           jnp.broadcast_to(ang_c[None, :, :], (rows, GRID_W, half))], axis=-1)
    ang = ang.reshape(rows * GRID_W, d_axis)
    return jnp.cos(ang), jnp.sin(ang)


def apply_rope(x, cos, sin):
    xf = x.astype(F32).reshape(*x.shape[:-1], HEAD_DIM // 2, 2)
    x1, x2 = xf[..., 0], xf[..., 1]
    cs, sn = cos[None, :, None, :], sin[None, :, None, :]
    out = jnp.stack([x1 * cs - x2 * sn, x1 * sn + x2 * cs], axis=-1).reshape(x.shape)
    return out.astype(x.dtype)


def project_q(h, w_q, q_g):
    b, L, _ = h.shape
    return rmsnorm((h @ w_q).reshape(b, L, N_HEADS, HEAD_DIM), q_g)


def project_kv(h, w_kv, k_g):
    b, L, _ = h.shape
    k, v = jnp.split((h @ w_kv).reshape(b, L, 2 * N_KV_HEADS, HEAD_DIM), 2, axis=2)
    return rmsnorm(k, k_g), v


def gqa_attend(q, k, v):
    s = jnp.einsum('bqkgd,bskd->bkgqs', q, k).astype(F32) * (HEAD_DIM ** -0.5)
    p = jax.nn.softmax(s, axis=-1).astype(v.dtype)
    return jnp.einsum('bkgqs,bskd->bqkgd', p, v)


def attention_mixer(a, ac, w_qkv, q_g, k_g, w_o, ctx_out):
    b, L, _ = a.shape
    w_q, w_kv = w_qkv[:, :N_HEADS * HEAD_DIM], w_qkv[:, N_HEADS * HEAD_DIM:]
    cos, sin = axial_rope_angles(L)
    q = apply_rope(project_q(a, w_q, q_g), cos, sin)
    k, v = project_kv(a, w_kv, k_g)
    k = apply_rope(k, cos, sin)
    kc, vc = project_kv(ac, w_kv, k_g)
    k_all = jnp.concatenate([k, kc], axis=1)
    v_all = jnp.concatenate([v, vc], axis=1)
    n_blk = L // Q_BLOCK
    qb = q.reshape(b, n_blk, Q_BLOCK, N_KV_HEADS, GQA_GROUP, HEAD_DIM).swapaxes(0, 1)
    o = lax.map(lambda q_blk: gqa_attend(q_blk, k_all, v_all), qb)
    y = o.swapaxes(0, 1).reshape(b, L, N_HEADS * HEAD_DIM) @ w_o
    if not ctx_out:
        return y, None
    lc = ac.shape[1]
    qc = project_q(ac, w_q, q_g).reshape(b, lc, N_KV_HEADS, GQA_GROUP, HEAD_DIM)
    yc = gqa_attend(qc, kc, vc).reshape(b, lc, N_HEADS * HEAD_DIM) @ w_o
    return y, yc


def setup_inputs(seed: int = 0) -> dict:
    key = jax.random.key(seed)
    n_a = len(range(0, DEPTH, N_MIXERS))
    n_b = len(range(1, DEPTH, N_MIXERS))
    n_c = len(range(2, DEPTH, N_MIXERS))
    keys = iter(jax.random.split(key, 40))

    def nrm(shape, scale):
        return jax.random.normal(next(keys), shape, F32) * scale

    d, f = D_MODEL, D_FF
    qkv_cols = (N_HEADS + 2 * N_KV_HEADS) * HEAD_DIM
    return {
        'x': nrm((BATCH, SEQ, d), 1.0),
        'c': nrm((BATCH, d), 1.0),
        'ctx': nrm((BATCH, CTX_LEN, d), 1.0),
        'c_ctx': nrm((d,), 1.0),
        'norm1_g': 1.0 + nrm((DEPTH, d), 0.02),
        'norm2_g': 1.0 + nrm((DEPTH, d), 0.02),
        'w_mod': nrm((DEPTH, d, N_MOD * d), 0.5 * d ** -0.5),
        'b_mod': nrm((DEPTH, N_MOD * d), 0.02),
        'ffn_w_gate': nrm((DEPTH, d, f), d ** -0.5),
        'ffn_w_up': nrm((DEPTH, d, f), d ** -0.5),
        'ffn_w_down': nrm((DEPTH, f, d), f ** -0.5),
        'hy_w_in': nrm((n_a, d, 3 * d), d ** -0.5),
        'hy_b_in': nrm((n_a, 3 * d), 0.02),
        'hy_short_w': nrm((n_a, 3, 3 * d), 3 ** -0.5),
        'hy_short_b': nrm((n_a, 3 * d), 0.02),
        'hy_f_w1': nrm((n_a, HYENA_EMB, HYENA_ORDER), HYENA_EMB ** -0.5),
        'hy_f_b1': nrm((n_a, HYENA_ORDER), 0.1),
        'hy_f_w2': nrm((n_a, HYENA_ORDER, HYENA_ORDER), HYENA_ORDER ** -0.5),
        'hy_f_b2': nrm((n_a, HYENA_ORDER), 0.1),
        'hy_f_w3': nrm((n_a, HYENA_ORDER, HYENA_ORDER), HYENA_ORDER ** -0.5),
        'hy_f_b3': nrm((n_a, HYENA_ORDER), 0.1),
        'hy_f_w4': nrm((n_a, HYENA_ORDER, 2 * d), 0.01),
        'hy_f_freq': 1.0 + nrm((n_a, HYENA_ORDER), 0.02),
        'hy_skip': nrm((n_a, d), 1.0),
        'hy_w_out': nrm((n_a, d, d), d ** -0.5),
        'hy_b_out': nrm((n_a, d), 0.02),
        'sc_w_in': nrm((n_b, d, 3 * d), d ** -0.5),
        'sc_conv_w': nrm((n_b, 3, d), 3 ** -0.5),
        'sc_w_out': nrm((n_b, d, d), d ** -0.5),
        'at_w_qkv': nrm((n_c, d, qkv_cols), d ** -0.5),
        'at_q_g': 1.0 + nrm((n_c, HEAD_DIM), 0.02),
        'at_k_g': 1.0 + nrm((n_c, HEAD_DIM), 0.02),
        'at_w_o': nrm((n_c, d, d), d ** -0.5),
        'final_g': 1.0 + nrm((d,), 0.02),
    }


def reference(x, c, ctx, c_ctx, norm1_g, norm2_g, w_mod, b_mod, ffn_w_gate, ffn_w_up,
              ffn_w_down, hy_w_in, hy_b_in, hy_short_w, hy_short_b, hy_f_w1, hy_f_b1,
              hy_f_w2, hy_f_b2, hy_f_w3, hy_f_b3, hy_f_w4, hy_f_freq, hy_skip, hy_w_out,
              hy_b_out, sc_w_in, sc_conv_w, sc_w_out, at_w_qkv, at_q_g, at_k_g, at_w_o,
              final_g):
    attn_layers = [i for i in range(DEPTH) if i % N_MIXERS == 2]
    last_ctx_reader = attn_layers[-1] if attn_layers else -1
    xc = ctx
    for i in range(DEPTH):
        kind, j = i % N_MIXERS, i // N_MIXERS
        ctx_in = i <= last_ctx_reader
        ctx_out = i < last_ctx_reader
        sh1, sc1, g1, sh2, sc2, g2 = adaln(c, w_mod[i], b_mod[i])
        a = modulate(rmsnorm(x, norm1_g[i]), sh1, sc1)
        if ctx_in:
            csh1, csc1, cg1, csh2, csc2, cg2 = adaln(c_ctx, w_mod[i], b_mod[i])
            ac = modulate(rmsnorm(xc, norm1_g[i]), csh1, csc1)
        yc = None
        if kind == 0:
            hyp = (hy_w_in[j], hy_b_in[j], hy_short_w[j], hy_short_b[j], hy_f_w1[j], hy_f_b1[j],
                   hy_f_w2[j], hy_f_b2[j], hy_f_w3[j], hy_f_b3[j], hy_f_w4[j], hy_f_freq[j],
                   hy_skip[j], hy_w_out[j], hy_b_out[j])
            y = hyena_mixer(a, *hyp)
            if ctx_out:
                yc = hyena_mixer(ac, *hyp)
        elif kind == 1:
            scp = (sc_w_in[j], sc_conv_w[j], sc_w_out[j])
            y = shortconv_mixer(a, *scp)
            if ctx_out:
                yc = shortconv_mixer(ac, *scp)
        else:
            y, yc = attention_mixer(a, ac, at_w_qkv[j], at_q_g[j], at_k_g[j], at_w_o[j], ctx_out)
        ffn = (ffn_w_gate[i], ffn_w_up[i], ffn_w_down[i])
        x = x + g1 * y
        x = x + g2 * swiglu(modulate(rmsnorm(x, norm2_g[i]), sh2, sc2), *ffn)
        if ctx_out:
            xc = xc + cg1 * yc
            xc = xc + cg2 * swiglu(modulate(rmsnorm(xc, norm2_g[i]), csh2, csc2), *ffn)
    return rmsnorm(x, final_g)
```

```python
import math
import numpy as np
from contextlib import ExitStack
import ml_dtypes
import concourse.bass as bass
import concourse.mybir as mybir
from concourse.bass_utils import run_bass_kernel_spmd

F32 = mybir.dt.float32
BF16 = mybir.dt.bfloat16
AF = mybir.ActivationFunctionType
ALU = mybir.AluOpType
NPBF = ml_dtypes.bfloat16

D = 2048
DC = 16
FF = 5632
FC = 44
NCORE = 8
SEQ = 16384
CTX = 256
TC = 4096
NB = 512
HD = 128
NH = 16
NKV = 4
EPS = 1e-6
LCN = 32768

ENGS = ("pe", "act", "dve", "pool", "sp")
SEM_LIMIT = 30000


class Buf:
    def __init__(self, name):
        self.name = name
        self.W = {}
        self.R = {}
        self.prev = {}
        self.dsems = []


class Prog:
    def __init__(self, nc, stack):
        self.nc = nc
        self.stack = stack
        self.q = {e: [] for e in ENGS}
        self.cnt = {e: 0 for e in ENGS}
        self.esems = {e: [] for e in ENGS}
        self.seen = {e: {} for e in ENGS}
        self.semh = {}
        self.nsem = 0
        self.final = {}
        self.nbuf = 0
        self.ntile = 0

    def new_sem(self, name):
        h = self.stack.enter_context(self.nc.semaphore(f"{name}_{self.nsem}"))
        key = self.nsem
        self.nsem += 1
        self.semh[key] = h
        return key

    def sbuf(self, name, shape, dt):
        self.ntile += 1
        return self.stack.enter_context(self.nc.sbuf_tensor(f"{name}_{self.ntile}", list(shape), dt))

    def psum(self, name, shape, dt=F32):
        self.ntile += 1
        return self.stack.enter_context(self.nc.psum_tensor(f"{name}_{self.ntile}", list(shape), dt))

    def buf(self, name=None):
        self.nbuf += 1
        return Buf(f"{name or 'b'}{self.nbuf}")

    def tile(self, name, shape, dt):
        return self.sbuf(name, shape, dt), self.buf(name)

    def _eng_token(self, e):
        i = self.cnt[e]
        r = i // SEM_LIMIT
        while len(self.esems[e]) <= r:
            self.esems[e].append(self.new_sem(f"c_{e}"))
        self.cnt[e] += 1
        return (self.esems[e][r], i % SEM_LIMIT + 1, 1)

    def _dma_token(self, b):
        if not b.dsems or b.dsems[-1][1] * 16 + 16 > SEM_LIMIT:
            b.dsems.append([self.new_sem("d"), 0])
        b.dsems[-1][1] += 1
        return (b.dsems[-1][0], b.dsems[-1][1] * 16, 16)

    @staticmethod
    def _merge(dst, src):
        for k, v in src.items():
            if dst.get(k, 0) < v:
                dst[k] = v

    def op(self, e, fn, reads=(), writes=(), appends=(), dma_on=None, final=False, nosync_self=False):
        need = {}
        for b in reads:
            self._merge(need, b.W)
        for b in writes:
            self._merge(need, b.W)
            self._merge(need, b.R)
        for b in appends:
            if b.R:
                pv = {}
                self._merge(pv, b.W)
                self._merge(pv, b.R)
                b.prev = pv
                b.W = {}
                b.R = {}
            self._merge(need, b.prev)
        tok = self._dma_token(dma_on) if dma_on is not None else self._eng_token(e)
        waits = []
        seen = self.seen[e]
        for k, v in need.items():
            if nosync_self and k in self.esems[e]:
                continue
            if seen.get(k, 0) < v:
                seen[k] = v
                waits.append((k, v))
        self.q[e].append((waits, fn, tok))
        t = {tok[0]: tok[1]}
        for b in reads:
            self._merge(b.R, t)
        for b in writes:
            b.W = dict(t)
            b.R = {}
            b.prev = {}
        for b in appends:
            self._merge(b.W, t)
        if final:
            self._merge(self.final, t)
        return tok

    def dma(self, q, out, in_, r=(), w=(), a=(), on=None, final=False):
        return self.op(q, lambda e: e.dma_start(out=out, in_=in_), reads=r, writes=w,
                       appends=a, dma_on=on, final=final)

    def mm(self, out, lhsT, rhs, start, stop, r, pbuf):
        if start:
            return self.op("pe", lambda e: e.matmul(out, lhsT=lhsT, rhs=rhs, start=True, stop=stop),
                           reads=r, writes=[pbuf], nosync_self=True)
        return self.op("pe", lambda e: e.matmul(out, lhsT=lhsT, rhs=rhs, start=False, stop=stop),
                       reads=r, appends=[pbuf], nosync_self=True)

    def act(self, out, in_, func, r=(), w=(), a=(), bias=None, scale=None):
        kw = {}
        if bias is not None:
            kw["bias"] = bias
        if scale is not None:
            kw["scale"] = scale
        return self.op("act", lambda e: e.activation(out=out, in_=in_, func=func, **kw), reads=r, writes=w, appends=a)

    def tt(self, eng, out, in0, in1, op, r=(), w=(), a=()):
        return self.op(eng, lambda e: e.tensor_tensor(out=out, in0=in0, in1=in1, op=op), reads=r, writes=w, appends=a)

    def ts(self, eng, out, in0, s1, s2, op0, op1=None, r=(), w=(), a=()):
        if s2 is None:
            return self.op(eng, lambda e: e.tensor_scalar(out=out, in0=in0, scalar1=s1, scalar2=None, op0=op0),
                           reads=r, writes=w, appends=a)
        return self.op(eng, lambda e: e.tensor_scalar(out=out, in0=in0, scalar1=s1, scalar2=s2, op0=op0, op1=op1),
                       reads=r, writes=w, appends=a)

    def stt(self, eng, out, in0, scalar, in1, op0, op1, r=(), w=(), a=()):
        return self.op(eng, lambda e: e.scalar_tensor_tensor(out=out, in0=in0, scalar=scalar, in1=in1, op0=op0, op1=op1),
                       reads=r, writes=w, appends=a)

    def copy(self, eng, out, in_, r=(), w=(), a=()):
        if eng == "act":
            return self.op(eng, lambda e: e.activation(out=out, in_=in_, func=AF.Copy), reads=r, writes=w, appends=a)
        return self.op(eng, lambda e: e.tensor_copy(out=out, in_=in_), reads=r, writes=w, appends=a)

    def recip(self, out, in_, r=(), w=(), a=()):
        return self.op("dve", lambda e: e.reciprocal(out=out, in_=in_), reads=r, writes=w, appends=a)

    def memset(self, eng, out, val, w=(), a=()):
        return self.op(eng, lambda e: e.memset(out, val), writes=w, appends=a)

    def finish(self, e="sp"):
        self.q[e].append((list(self.final.items()), None, None))

    def emit(self):
        nc = self.nc
        semh = self.semh
        q = self.q

        def run(eng, lst):
            for waits, fn, tok in lst:
                for k, v in waits:
                    eng.wait_ge(semh[k], v)
                if fn is not None:
                    fn(eng).then_inc(semh[tok[0]], tok[2])

        with nc.Block() as block:
            @block.tensor
            def _(eng):
                run(eng, q["pe"])

            @block.scalar
            def _(eng):
                run(eng, q["act"])

            @block.vector
            def _(eng):
                run(eng, q["dve"])

            @block.gpsimd
            def _(eng):
                run(eng, q["pool"])

            @block.sync
            def _(eng):
                run(eng, q["sp"])


class Ring:
    def __init__(self, P, name, n, shape, dt, psum=False):
        self.t = [(P.psum(name, shape, dt) if psum else P.sbuf(name, shape, dt)) for _ in range(n)]
        self.b = [P.buf(name) for _ in range(n)]
        self.i = 0
        self.n = n

    def next(self):
        k = self.i % self.n
        self.i += 1
        return self.t[k], self.b[k]


class PV:
    def __init__(self):
        self.cols = []
        self.off = {}
        self.n = 0

    def add(self, name, arr):
        arr = np.asarray(arr, dtype=np.float32)
        assert arr.shape[0] == 128, (name, arr.shape)
        arr = arr.reshape(128, -1)
        self.off[name] = self.n
        self.n += arr.shape[1]
        self.cols.append(arr)

    def add_vec(self, name, v):
        v = np.asarray(v, dtype=np.float32)
        self.add(name, v.reshape(-1, 128).T)

    def array(self):
        return np.ascontiguousarray(np.concatenate(self.cols, axis=1))


class Ctx:
    pass


def new_prog(nc, st, wslots=2, wsize=11264, npsum=6):
    nc.allow_low_precision("bf16 matmul operands with fp32 accumulation")
    P = Prog(nc, st)
    c = Ctx()
    c.P = P
    c.wring = Ring(P, "w", wslots, [128, wsize], BF16)
    c.pring = Ring(P, "ps", npsum, [128, 512], F32, psum=True)
    c.aux = Ring(P, "pa", 8 - npsum, [128, 512], F32, psum=True)
    c.ones_d, c.ones_db = P.tile("ones_d", [128, 128], BF16)
    c.ones_h, c.ones_hb = P.tile("ones_h", [128, 128], BF16)
    c.ones_1, c.ones_1b = P.tile("ones_1", [128, 128], BF16)
    P.memset("dve", c.ones_d[:], 1.0 / 2048.0, w=[c.ones_db])
    P.memset("dve", c.ones_h[:], 1.0 / 128.0, w=[c.ones_hb])
    P.memset("dve", c.ones_1[:], 1.0, w=[c.ones_1b])
    return P, c


def load_pv(P, c, pv_ap, pvobj):
    c.pv, c.pvb = P.tile("pv", [128, pvobj.n], F32)
    c.pvo = pvobj.off
    P.dma("sp", c.pv[:], pv_ap[:, :], w=[c.pvb], on=c.pvb)


def pvc(c, name, i=0, n=1):
    o = c.pvo[name] + i
    return c.pv[:, o:o + n]


def linear(P, c, W, K, M, rhs_list, epi, mw=512, order=None):
    KC = K // 128
    Wv = W.rearrange("(kc p) m -> p kc m", p=128)
    if order is not None:
        groups = [[mc] for mc in order]
    else:
        groups = [list(range(m0 // 128, min(M, m0 + mw) // 128)) for m0 in range(0, M, mw)]
    for grp in groups:
        w = 128 * len(grp)
        m0 = grp[0] * 128
        wt, wb = c.wring.next()
        wv = wt[:, 0:KC * w].rearrange("p (k m) -> p k m", k=KC)
        P.dma("pool", wv, Wv[:, :, m0:m0 + w], w=[wb], on=wb)
        for j, mc in enumerate(grp):
            for ri, (h, n, hb) in enumerate(rhs_list):
                ps, pb = c.pring.next()
                for kc in range(KC):
                    P.mm(ps[:, :n], wv[:, kc, j * 128:(j + 1) * 128], h[:, kc, :n], kc == 0, kc == KC - 1,
                         [wb, hb], pb)
                epi(mc, ri, ps, pb)


def alloc_norm(P, c, n=NB):
    c.rs, c.rsb = P.tile("rs", [128, n], F32)
    c.tring = Ring(P, "t", 3, [128, n + 2], F32)


def norm_mod(P, c, x, xb, n, ge, sh, gb, h, hb):
    P.act(h[:, :, :n], x[:, :, :n], AF.Square, r=[xb], w=[hb])
    ps, pb = c.aux.next()
    for kc in range(DC):
        P.mm(ps[:, :n], c.ones_d[:, :], h[:, kc, :n], kc == 0, kc == DC - 1, [c.ones_db, hb], pb)
    P.act(c.rs[:, :n], ps[:, :n], AF.Sqrt, r=[pb, c.pvb], w=[c.rsb], bias=pvc(c, "eps"))
    P.recip(c.rs[:, :n], c.rs[:, :n], w=[c.rsb])
    for kc in range(DC):
        t, tb = c.tring.next()
        P.stt("dve", t[:, :n], x[:, kc, :n], ge[:, kc:kc + 1], c.rs[:, :n], ALU.mult, ALU.mult,
              r=[xb, gb, c.rsb], w=[tb])
        if kc == 0:
            P.act(h[:, kc, :n], t[:, :n], AF.Identity, r=[tb, gb], w=[hb], bias=sh[:, kc:kc + 1])
        else:
            P.act(h[:, kc, :n], t[:, :n], AF.Identity, r=[tb, gb], a=[hb], bias=sh[:, kc:kc + 1])


def adaln(P, c, wmod, cc_ap, bmod_name, g1_name=None, g2_name=None, tag="", mw=512):
    cct, cctb = P.tile("cct" + tag, [128, DC, 2], F32)
    ccb, ccbb = P.tile("ccb" + tag, [128, DC, 2], BF16)
    mod, modb = P.tile("mod" + tag, [128, 2, 96], F32)
    P.dma("sp", cct[:], cc_ap[:, :, :], w=[cctb], on=cctb)
    P.act(ccb[:], cct[:], AF.Silu, r=[cctb], w=[ccbb])
    first = [True]

    def epi(mc, ri, ps, pb):
        for s in range(2):
            P.act(mod[:, s, mc:mc + 1], ps[:, s:s + 1], AF.Identity, r=[pb, c.pvb],
                  bias=pvc(c, bmod_name, mc), **({"w": [modb]} if first[0] else {"a": [modb]}))
            first[0] = False
    linear(P, c, wmod, D, 6 * D, [(ccb, 2, ccbb)], epi, mw=mw)
    ge, geb = P.tile("ge" + tag, [128, 2, 2, DC], F32)
    res = {"buf": modb, "geb": geb}
    for s in range(2):
        for k, gname in enumerate((g1_name, g2_name)):
            if gname is None:
                continue
            sc = mod[:, s, 16 + 48 * k:32 + 48 * k]
            P.stt("dve", ge[:, s, k, :], sc, 1.0, pvc(c, gname, 0, DC), ALU.add, ALU.mult, r=[modb, c.pvb],
                  **({"w": [geb]} if (s == 0 and k == 0) else {"a": [geb]}))
        res[s] = dict(sh1=mod[:, s, 0:16], ge1=ge[:, s, 0, :], gt1=mod[:, s, 32:48],
                      sh2=mod[:, s, 48:64], ge2=ge[:, s, 1, :], gt2=mod[:, s, 80:96])
    res["mb"] = [modb, geb]
    return res


def alloc_ffn(P, c, n=NB):
    c.actt, c.acttb = P.tile("actt", [128, FC, n], BF16)


def ffn(P, c, x, xb, h, hb, n, wg, wu, wd, gt2, modb):
    first = [True]

    def epi_g(mc, ri, ps, pb):
        P.act(c.actt[:, mc, :n], ps[:, :n], AF.Silu, r=[pb], **({"w": [c.acttb]} if first[0] else {"a": [c.acttb]}))
        first[0] = False
    linear(P, c, wg, D, FF, [(h, n, hb)], epi_g)

    def epi_u(mc, ri, ps, pb):
        P.tt("dve", c.actt[:, mc, :n], ps[:, :n], c.actt[:, mc, :n], ALU.mult, r=[pb], w=[c.acttb])
    linear(P, c, wu, D, FF, [(h, n, hb)], epi_u)

    def epi_d(mc, ri, ps, pb):
        P.stt("dve", x[:, mc, :n], ps[:, :n], gt2[:, mc:mc + 1], x[:, mc, :n], ALU.mult, ALU.add,
              r=[pb, modb], w=[xb])
    linear(P, c, wd, FF, D, [(c.actt, n, c.acttb)], epi_d, mw=256)


def dram_in(nc, name, shape, dt=F32):
    return nc.dram_tensor(name, list(shape), dt, kind="ExternalInput").ap()


def dram_out(nc, name, shape, dt=F32):
    return nc.dram_tensor(name, list(shape), dt, kind="ExternalOutput").ap()


TWO_PI = 2.0 * math.pi


def alloc_filter(P, c):
    f = Ctx()
    f.hA, f.hAb = P.tile("fhA", [64, 512], F32)
    f.hB, f.hBb = P.tile("fhB", [64, 512], F32)
    f.dec, f.decb = P.tile("fdec", [128, 512], F32)
    f.dcb, f.dcbb = P.tile("fdcb", [128, 512], F32)
    f.zt, f.ztb = P.tile("fzt", [33, 512], F32)
    f.tr, f.trb = P.tile("ftr", [128, 512], F32)
    f.m0, f.m0b = P.tile("fm0", [128, 512], F32)
    f.kfr = Ring(P, "fkf", 2, [128, 512], F32)
    f.kbr = Ring(P, "fkb", 2, [128, 512], F32)
    f.red, f.redb = P.tile("fred", [64, 512], F32)
    f.fc, f.fcb = P.tile("ffc", [64, 4], F32)
    o = c.pvo
    P.ts("dve", f.fc[:, 0:1], c.pv[0:64, o["ffr"]:o["ffr"] + 1], 1.0 / TWO_PI, None, ALU.mult, r=[c.pvb], w=[f.fcb])
    for k_, nm_ in enumerate(("fb1", "fb2", "fb3")):
        P.tt("dve", f.fc[:, k_ + 1:k_ + 2], c.pv[0:64, o[nm_]:o[nm_] + 1], f.fc[:, 0:1], ALU.mult, r=[c.pvb], w=[f.fcb])
    c.flt = f


def filter_mlp(P, c, npos, zT, trow, m0row, fw, fwb, out_cb):
    f = c.flt
    w1, w2, w3, w4 = fw
    for g0 in range(0, npos, 512):
        n = min(512, npos - g0)
        P.dma("sp", f.zt[:, :n], zT[:, g0:g0 + n], w=[f.ztb], on=f.ztb)
        P.dma("sp", f.tr[:, :n], trow[:, g0:g0 + n], w=[f.trb], on=f.trb)
        P.dma("sp", f.m0[:, :n], m0row[:, g0:g0 + n], w=[f.m0b], on=f.m0b)
        src, srcb, kk = f.zt[0:33, :n], f.ztb, 33
        for li, (wl, bname) in enumerate(((w1, "fb1"), (w2, "fb2"), (w3, "fb3"))):
            ps, pb = c.aux.next()
            P.mm(ps[0:64, :n], wl[0:kk, 0:64], src, True, True, [fwb, srcb], pb)
            dst, dstb = (f.hA, f.hAb) if li % 2 == 0 else (f.hB, f.hBb)
            P.act(dst[:, :n], ps[0:64, :n], AF.Identity, r=[pb, f.fcb], w=[dstb],
                  scale=f.fc[0:64, 0:1], bias=f.fc[0:64, li + 1:li + 2])
            r_, rb_ = f.red, f.redb
            P.stt("dve", r_[:, :n], dst[:, :n], 0.5, dst[:, :n], ALU.is_gt, ALU.subtract, r=[dstb], w=[rb_])
            P.stt("dve", r_[:, :n], dst[:, :n], 1.5, r_[:, :n], ALU.is_gt, ALU.add, r=[dstb], w=[rb_])
            P.stt("dve", r_[:, :n], dst[:, :n], -0.5, r_[:, :n], ALU.is_ge, ALU.add, r=[dstb], w=[rb_])
            P.stt("dve", r_[:, :n], dst[:, :n], -1.5, r_[:, :n], ALU.is_ge, ALU.add, r=[dstb], w=[rb_])
            P.act(dst[:, :n], r_[:, :n], AF.Sin, r=[rb_, c.pvb], w=[dstb], scale=-TWO_PI,
                  bias=c.pv[0:64, c.pvo["fourpi"]:c.pvo["fourpi"] + 1])
            src, srcb, kk = dst[0:64, :n], dstb, 64
        for ch in range(DC):
            P.act(f.dec[:, :n], f.tr[:, :n], AF.Exp, r=[f.trb, c.pvb], w=[f.decb], scale=pvc(c, "nad", ch))
            P.tt("pool", f.dcb[:, :n], f.dec[:, :n], f.m0[:, :n], ALU.mult, r=[f.decb, f.m0b], w=[f.dcbb])
            psf, pfb = c.pring.next()
            P.mm(psf[:, :n], w4[0:64, ch * 128:(ch + 1) * 128], src, True, True, [fwb, srcb], pfb)
            psb, pbb = c.pring.next()
            P.mm(psb[:, :n], w4[0:64, D + ch * 128:D + (ch + 1) * 128], src, True, True, [fwb, srcb], pbb)
            kf, kfb = f.kfr.next()
            kb, kbb = f.kbr.next()
            P.tt("dve", kf[:, :n], psf[:, :n], f.dec[:, :n], ALU.mult, r=[pfb, f.decb], w=[kfb])
            P.tt("dve", kb[:, :n], psb[:, :n], f.dcb[:, :n], ALU.mult, r=[pbb, f.dcbb], w=[kbb])
            out_cb(ch, g0, n, kf, kfb, kb, kbb)


def build_A(pvo, npv, with_ctx, nblk=8, npos=2048, nb=NB):
    nc = bass.Bass("TRN2", target_bir_lowering=False)
    T = nblk * nb
    nba = nblk + (1 if with_ctx else 0)
    xT = dram_in(nc, "xT", [D, T])
    xh = dram_in(nc, "xh", [128, nba, DC, 2])
    hmask = dram_in(nc, "hmask", [128, nba, 2])
    cc = dram_in(nc, "cc", [128, DC, 2])
    pv_d = dram_in(nc, "pv", [128, npv])
    wmod = dram_in(nc, "wmod", [D, 6 * D])
    win = dram_in(nc, "win", [D, 3 * D])
    zT = dram_in(nc, "zT", [33, npos])
    trow = dram_in(nc, "trow", [128, npos])
    m0row = dram_in(nc, "m0row", [128, npos])
    fw1 = dram_in(nc, "fw1", [33, 64])
    fw2 = dram_in(nc, "fw2", [64, 64])
    fw3 = dram_in(nc, "fw3", [64, 64])
    fw4 = dram_in(nc, "fw4", [64, 2 * D])
    vxT = dram_out(nc, "vxT", [D, T])
    x0T = dram_out(nc, "x0T", [D, T])
    ksT = dram_out(nc, "ksT", [D, npos])
    kdT = dram_out(nc, "kdT", [D, npos])
    if with_ctx:
        cxT = dram_in(nc, "cxT", [D, CTX])
        zTc = dram_in(nc, "zTc", [33, CTX])
        trowc = dram_in(nc, "trowc", [128, CTX])
        m0rowc = dram_in(nc, "m0rowc", [128, CTX])
        cvxT = dram_out(nc, "cvxT", [D, CTX])
        cx0T = dram_out(nc, "cx0T", [D, CTX])
        cyT = dram_out(nc, "cyT", [D, CTX])
    pvobj = PV()
    pvobj.off, pvobj.n = pvo, npv
    with ExitStack() as st:
        P, c = new_prog(nc, st, wsize=4096)
        load_pv(P, c, pv_d, pvobj)
        alloc_norm(P, c, nb)
        alloc_filter(P, c)
        md = adaln(P, c, wmod, cc, "bmod", g1_name="g1", mw=256)
        fwt, fwb = [], P.buf("fw")
        for nm, ap_, shp in (("fw1", fw1, [33, 64]), ("fw2", fw2, [64, 64]), ("fw3", fw3, [64, 64]),
                             ("fw4", fw4, [64, 2 * D])):
            t = P.sbuf(nm, shp, F32)
            P.dma("sp", t[:], ap_[:, :], a=[fwb], on=fwb)
            fwt.append(t)
        ksr = Ring(P, "ks", 2, [128, 512], F32)
        kdr = Ring(P, "kd", 2, [128, 512], F32)

        def out_main(ch, g0, n, kf, kfb, kb, kbb):
            ks, ksb = ksr.next()
            kd, kdb = kdr.next()
            P.tt("pool", ks[:, :n], kf[:, :n], kb[:, :n], ALU.add, r=[kfb, kbb], w=[ksb])
            P.tt("pool", kd[:, :n], kf[:, :n], kb[:, :n], ALU.subtract, r=[kfb, kbb], w=[kdb])
            P.dma("sp", ksT[ch * 128:(ch + 1) * 128, g0:g0 + n], ks[:, :n], r=[ksb], on=ksb, final=True)
            P.dma("sp", kdT[ch * 128:(ch + 1) * 128, g0:g0 + n], kd[:, :n], r=[kdb], on=kdb, final=True)
        filter_mlp(P, c, npos, zT, trow, m0row, fwt, fwb, out_main)
        if with_ctx:
            ckf = P.sbuf("ckf", [128, DC, CTX], F32)
            ckb = P.sbuf("ckb", [128, DC, CTX], F32)
            ckbufs = [P.buf("ck") for _ in range(DC)]

            def out_ctx(ch, g0, n, kf, kfb, kb, kbb):
                P.copy("pool", ckf[:, ch, :], kf[:, :n], r=[kfb], w=[ckbufs[ch]])
                P.copy("pool", ckb[:, ch, :], kb[:, :n], r=[kbb], a=[ckbufs[ch]])
            filter_mlp(P, c, CTX, zTc, trowc, m0rowc, fwt, fwb, out_ctx)
            cvx = P.sbuf("cvx", [128, DC, CTX], F32)
            cvxb = [P.buf("cvx") for _ in range(DC)]
            ctmp, ctmpb = P.tile("ctmp", [128, CTX], F32)
            cy = P.sbuf("cy", [128, DC, CTX], F32)
            cyb = [P.buf("cy") for _ in range(DC)]
        hm, hmb = P.tile("hm", [128, nba, 2], F32)
        P.dma("sp", hm[:], hmask[:, :, :], w=[hmb], on=hmb)
        x, xb = P.tile("x", [128, DC, nb], F32)
        xht, xhb = P.tile("xht", [128, DC, 2], F32)
        h, hb = P.tile("h", [128, DC, nb], BF16)
        hh, hhb = P.tile("hh", [128, DC, 2], BF16)
        uring = Ring(P, "ue", 3, [128, nb + 2], F32)
        yring = Ring(P, "y", 4, [128, nb], F32)
        order = []
        for jx in range(DC):
            order += [DC + jx, 2 * DC + jx, jx]
        for k in range(nba):
            isctx = k >= nblk
            s = 1 if isctx else 0
            n = CTX if isctx else nb
            src = cxT[:, :] if isctx else xT[:, k * nb:(k + 1) * nb]
            P.dma("sp", x[:, :, :n], src.rearrange("(c p) t -> p c t", p=128), w=[xb], on=xb)
            P.dma("sp", xht[:], xh[:, k, :, :], w=[xhb], on=xhb)
            m = md[s]
            norm_mod(P, c, x, xb, n, m["ge1"], m["sh1"], md["geb"], h, hb)
            norm_mod(P, c, xht, xhb, 2, m["ge1"], m["sh1"], md["geb"], hh, hhb)
            cur = {}
            x1c = {}

            def epi(mc, ri, ps, pb, n=n, k=k, isctx=isctx, cur=cur, x1c=x1c):
                bcol = pvc(c, "bin", mc)
                if ri == 0:
                    ue, ueb = uring.next()
                    cur[mc] = (ue, ueb)
                    P.act(ue[:, 1:n + 1], ps[:, :n], AF.Identity, r=[pb, c.pvb], w=[ueb], bias=bcol)
                    return
                ue, ueb = cur.pop(mc)
                P.stt("dve", ue[:, 0:1], ps[:, 0:1], bcol, hm[:, k, 0:1], ALU.add, ALU.mult,
                      r=[pb, c.pvb, hmb], a=[ueb])
                P.stt("dve", ue[:, n + 1:n + 2], ps[:, 1:2], bcol, hm[:, k, 1:2], ALU.add, ALU.mult,
                      r=[pb, c.pvb, hmb], a=[ueb])
                y, yb = yring.next()
                P.act(y[:, :n], ue[:, 1:n + 1], AF.Identity, r=[ueb, c.pvb], w=[yb],
                      scale=pvc(c, "sw1", mc), bias=pvc(c, "sb", mc))
                P.stt("dve", y[:, :n], ue[:, 0:n], pvc(c, "sw0", mc), y[:, :n], ALU.mult, ALU.add,
                      r=[ueb, c.pvb], w=[yb])
                P.stt("dve", y[:, :n], ue[:, 2:n + 2], pvc(c, "sw2", mc), y[:, :n], ALU.mult, ALU.add,
                      r=[ueb, c.pvb], w=[yb])
                kind, ch = mc // DC, mc % DC
                rows = slice(ch * 128, (ch + 1) * 128)
                if kind == 1:
                    x1c[ch] = (y, yb)
                elif kind == 2:
                    y1, y1b = x1c.pop(ch)
                    if isctx:
                        P.tt("pool", cvx[:, ch, :], y[:, :n], y1[:, :n], ALU.mult, r=[yb, y1b], w=[cvxb[ch]])
                        P.dma("sp", cvxT[rows, :], cvx[:, ch, :], r=[cvxb[ch]], on=cvxb[ch], final=True)
                    else:
                        P.tt("pool", y[:, :n], y[:, :n], y1[:, :n], ALU.mult, r=[y1b], w=[yb])
                        P.dma("sp", vxT[rows, k * nb:k * nb + n], y[:, :n], r=[yb], on=yb, final=True)
                else:
                    dst = cx0T[rows, :] if isctx else x0T[rows, k * nb:k * nb + n]
                    P.dma("sp", dst, y[:, :n], r=[yb], on=yb, final=True)
            linear(P, c, win, D, 3 * D, [(h, n, hb), (hh, 2, hhb)], epi, order=order)
        if with_ctx:
            for ch in range(DC):
                eng = "dve" if ch % 2 == 0 else "pool"
                v = cvx[:, ch, :]
                yy = cy[:, ch, :]
                rd = [cvxb[ch], ckbufs[ch]]
                P.ts(eng, yy[:, 0:CTX], v[:, 0:CTX], ckf[:, ch, 0:1], None, ALU.mult, r=rd, w=[cyb[ch]])
                for d in range(1, CTX):
                    for (o_, i_, kk_) in ((yy[:, d:CTX], v[:, 0:CTX - d], ckf[:, ch, d:d + 1]),
                                          (yy[:, 0:CTX - d], v[:, d:CTX], ckb[:, ch, d:d + 1])):
                        if eng == "dve":
                            P.stt(eng, o_, i_, kk_, o_, ALU.mult, ALU.add, r=rd, w=[cyb[ch]])
                        else:
                            P.ts(eng, ctmp[:, 0:CTX - d], i_, kk_, None, ALU.mult, r=rd, w=[ctmpb])
                            P.tt(eng, o_, o_, ctmp[:, 0:CTX - d], ALU.add, r=[ctmpb], w=[cyb[ch]])
                P.dma("sp", cyT[ch * 128:(ch + 1) * 128, :], cy[:, ch, :], r=[cyb[ch]], on=cyb[ch], final=True)
        P.finish("sp")
        P.emit()
    return nc


def fm(v):
    v = np.asarray(v, dtype=np.float32)
    return np.ascontiguousarray(v.reshape(-1, 128).T)


def rep128(row):
    row = np.asarray(row, dtype=np.float32)
    return np.ascontiguousarray(np.broadcast_to(row[None, :], (128, row.shape[0])))


def hyena_z(L):
    t = np.linspace(0.0, 1.0, L, dtype=np.float32)[:, None]
    bands = 16
    omega = (np.float32(2.0 * math.pi) * np.arange(L, dtype=np.float32)[:, None] / np.float32(L)).astype(np.float32)
    fb = np.linspace(1e-4, bands - 1, bands, dtype=np.float32)[None, :]
    z = np.concatenate([t, np.cos(fb * omega), -np.sin(fb * omega)], axis=-1).astype(np.float32)
    return z, t[:, 0]


def hyena_nad():
    max_decay = math.log(1e-2) / 0.3
    min_decay = math.log(1e-2) / 1.5
    deltas = np.linspace(min_decay, max_decay, D, dtype=np.float32)
    return -np.abs(deltas)


def col64(v):
    c_ = np.zeros((128, 1), np.float32)
    c_[:64, 0] = v
    return c_


def cc_arr(c_b, c_ctx):
    return np.ascontiguousarray(np.stack([fm(c_b), fm(c_ctx)], axis=-1))


def halo_arrays(x_full_b, t0, nblk, nb, with_ctx):
    L = x_full_b.shape[0]
    nba = nblk + (1 if with_ctx else 0)
    xh = np.zeros((128, nba, DC, 2), np.float32)
    hm = np.zeros((128, nba, 2), np.float32)
    for k in range(nblk):
        for side, t in enumerate((t0 + k * nb - 1, t0 + (k + 1) * nb)):
            if 0 <= t < L:
                xh[:, k, :, side] = fm(x_full_b[t])
                hm[:, k, side] = 1.0
    return xh, hm


def pv_A(inp, li):
    j = li // 3
    pv = PV()
    pv.add("eps", np.full((128, 1), EPS, np.float32))
    pv.add("fourpi", np.full((128, 1), 4.0 * math.pi, np.float32))
    pv.add_vec("bmod", inp["b_mod"][li])
    pv.add_vec("g1", inp["norm1_g"][li])
    pv.add_vec("bin", inp["hy_b_in"][j])
    for i in range(3):
        pv.add_vec(f"sw{i}", inp["hy_short_w"][j][i])
    pv.add_vec("sb", inp["hy_short_b"][j])
    pv.add("fb1", col64(inp["hy_f_b1"][j]))
    pv.add("fb2", col64(inp["hy_f_b2"][j]))
    pv.add("fb3", col64(inp["hy_f_b3"][j]))
    pv.add("ffr", col64(inp["hy_f_freq"][j]))
    pv.add_vec("nad", hyena_nad())
    return pv


def maps_A(inp, li, x_full, xc_full, ncore, nblk, nb, L, with_ctx):
    j = li // 3
    pv = pv_A(inp, li)
    pva = pv.array()
    T = nblk * nb
    cpb = L // T
    npos = L * 1 // ncore if ncore * T >= L else L // ncore
    npos = L // ncore
    z, t = hyena_z(L)
    zc, tc = hyena_z(CTX)
    maps = []
    for ci in range(ncore):
        b, t0 = ci // cpb, (ci % cpb) * T
        xh, hm = halo_arrays(x_full[b], t0, nblk, nb, with_ctx)
        p0 = ci * npos
        m0 = np.ones(npos, np.float32)
        if p0 == 0:
            m0[0] = 0.0
        m = {
            "xT": np.ascontiguousarray(x_full[b, t0:t0 + T].T), "xh": xh, "hmask": hm,
            "cc": cc_arr(inp["c"][b], inp["c_ctx"]), "pv": pva,
            "wmod": inp["w_mod"][li], "win": inp["hy_w_in"][j],
            "zT": np.ascontiguousarray(z[p0:p0 + npos].T), "trow": rep128(t[p0:p0 + npos]), "m0row": rep128(m0),
            "fw1": inp["hy_f_w1"][j], "fw2": inp["hy_f_w2"][j], "fw3": inp["hy_f_w3"][j], "fw4": inp["hy_f_w4"][j],
        }
        if with_ctx:
            m0c = np.ones(CTX, np.float32)
            m0c[0] = 0.0
            m.update({"cxT": np.ascontiguousarray(xc_full[b].T), "zTc": np.ascontiguousarray(zc.T),
                      "trowc": rep128(tc), "m0rowc": rep128(m0c)})
        maps.append(m)
    return pv, maps, npos


def run(nc, maps):
    res = run_bass_kernel_spmd(nc, maps, core_ids=list(range(len(maps))))
    return res.results


def lc_tables():
    N = LCN
    n1 = np.arange(64)[:, None]
    k1 = np.arange(128)[None, :]
    a = 2 * np.pi * n1 * k1 / 128.0
    F1 = np.concatenate([np.cos(a), -np.sin(a)], axis=1)
    n2 = np.arange(256)[:, None]
    ph = 2 * np.pi * n2 * k1 / N
    TA = np.concatenate([np.cos(ph), np.sin(ph)], axis=1)
    TB = np.concatenate([-np.sin(ph), np.cos(ph)], axis=1)
    k2 = np.arange(256)[None, :]
    b2 = 2 * np.pi * n2 * k2 / 256.0
    C2, S2 = np.cos(b2), np.sin(b2)
    RA = np.concatenate([C2, S2], axis=1)
    RB = np.concatenate([-S2, C2], axis=1)
    th = (2 * np.pi * n2 * k1 / N).T
    UA = np.concatenate([np.cos(th), -np.sin(th)], axis=1)
    UB = np.concatenate([np.sin(th), np.cos(th)], axis=1)
    a2 = (2 * np.pi * n1 * k1 / 128.0).T
    C1i, NS1i = np.cos(a2) / N, -np.sin(a2) / N

    def halves(m):
        return np.ascontiguousarray(m.reshape(2, 128, -1).transpose(1, 0, 2)).astype(np.float32)
    f = np.float32
    return {"F1": F1.astype(f), "TA": halves(TA), "TB": halves(TB), "C2": halves(C2), "S2": halves(S2),
            "NS2": halves(-S2), "RA": halves(RA), "RB": halves(RB), "UA": UA.astype(f), "UB": UB.astype(f),
            "C1i": C1i.astype(f), "NS1i": NS1i.astype(f)}


def build_B(CH=256, G=8):
    nc = bass.Bass("TRN2", target_bir_lowering=False)
    XT = dram_in(nc, "XT", [64, CH, 4, 256])
    shapes = {"F1": [64, 256], "TA": [128, 2, 256], "TB": [128, 2, 256], "C2": [128, 2, 256], "S2": [128, 2, 256],
              "NS2": [128, 2, 256], "RA": [128, 2, 512], "RB": [128, 2, 512], "UA": [128, 512], "UB": [128, 512],
              "C1i": [128, 64], "NS1i": [128, 64]}
    td = {k: dram_in(nc, k, v) for k, v in shapes.items()}
    yo = dram_out(nc, "yo", [64, CH, 2, 256])
    with ExitStack() as st:
        nc.allow_low_precision("bf16 matmul operands with fp32 accumulation")
        P = Prog(nc, st)
        tb = P.buf("tables")
        T = {}
        for k, shp in shapes.items():
            isf = k in ("TA", "TB", "UA", "UB")
            T[k] = P.sbuf(k, shp, F32 if isf else BF16)
            src = td[k]
            full = tuple(slice(None) for _ in shp)
            P.dma("sp" if isf else "pool", T[k][full], src[full], a=[tb], on=tb)
        pring = Ring(P, "ps", 8, [128, 512], F32, psum=True)
        xin = Ring(P, "xin", 2, [64, G, 4, 256], BF16)
        Br, Brb = P.tile("Br", [128, 2, 4 * G, 128], BF16)
        Bi, Bib = P.tile("Bi", [128, 2, 4 * G, 128], BF16)
        Yr, Yrb = P.tile("Yr", [128, 2, G, 2, 128], BF16)
        Yi, Yib = P.tile("Yi", [128, 2, G, 2, 128], BF16)
        Zr, Zrb = P.tile("Zr", [128, G, 2, 256], BF16)
        Zi, Zib = P.tile("Zi", [128, G, 2, 256], BF16)
        yout = Ring(P, "yout", 2, [64, G, 2, 256], F32)
        p1r = Ring(P, "p1", 3, [128, 512], F32)
        p2r = Ring(P, "p2", 3, [128, 512], F32)
        kkr = Ring(P, "kk", 2, [128, 2, 2, 128], F32)
        for g in range(CH // G):
            xt, xtb = xin.next()
            P.dma("pool", xt[:, :, :, :], XT[:, g * G:(g + 1) * G, :, :], w=[xtb], on=xtb)
            firstB = True
            for ch in range(G):
                for half in range(2):
                    for cp in range(2):
                        ps, pb = pring.next()
                        for j in range(2):
                            col = 2 * cp + j
                            P.op("pe", (lambda e, o=ps[:, j * 256:(j + 1) * 256],
                                        l=xt[0:64, ch, col, half * 128:(half + 1) * 128], r_=T["F1"][0:64, :]:
                                        e.matmul(o, lhsT=l, rhs=r_, start=True, stop=True)),
                                 reads=[xtb, tb], **({"writes": [pb]} if j == 0 else {"appends": [pb]}),
                                 nosync_self=True)
                        p1, p1b = p1r.next()
                        p2, p2b = p2r.next()
                        for j in range(2):
                            P.tt("dve", p1[:, j * 256:(j + 1) * 256], ps[:, j * 256:(j + 1) * 256], T["TA"][:, half, :],
                                 ALU.mult, r=[pb, tb], **({"w": [p1b]} if j == 0 else {"a": [p1b]}))
                            P.tt("dve", p2[:, j * 256:(j + 1) * 256], ps[:, j * 256:(j + 1) * 256], T["TB"][:, half, :],
                                 ALU.mult, r=[pb, tb], **({"w": [p2b]} if j == 0 else {"a": [p2b]}))
                        c0 = 4 * ch + 2 * cp
                        p1v = p1[:, :].rearrange("p (j x) -> p j x", j=2)
                        p2v = p2[:, :].rearrange("p (j x) -> p j x", j=2)
                        kw = {"w": [Brb]} if firstB else {"a": [Brb]}
                        P.tt("pool", Br[:, half, c0:c0 + 2, :], p1v[:, :, 0:128], p1v[:, :, 128:256], ALU.add,
                             r=[p1b], **kw)
                        kw = {"w": [Bib]} if firstB else {"a": [Bib]}
                        P.tt("pool", Bi[:, half, c0:c0 + 2, :], p2v[:, :, 0:128], p2v[:, :, 128:256], ALU.add,
                             r=[p2b], **kw)
                        firstB = False
            firstY = True
            for ch in range(G):
                for k2c in range(2):
                    ks_ = slice(k2c * 128, (k2c + 1) * 128)
                    psR, pRb = pring.next()
                    psI, pIb = pring.next()
                    rb_ = lambda t_, hf: t_[:, hf, 4 * ch:4 * ch + 4, :].rearrange("p c k -> p (c k)")
                    seq = [(psR, pRb, "C2", Br, Brb), (psR, pRb, "S2", Bi, Bib),
                           (psI, pIb, "C2", Bi, Bib), (psI, pIb, "NS2", Br, Brb)]
                    for si, (po, pob, tn, bt, btb) in enumerate(seq):
                        for hf in range(2):
                            P.mm(po[:, :], T[tn][:, hf, ks_], rb_(bt, hf), (si % 2 == 0 and hf == 0),
                                 (si % 2 == 1 and hf == 1), [tb, btb], pob)
                    kk, kkb = kkr.next()
                    P.copy("act", kk[:, 0, 0, :], psR[:, 256:384], r=[pRb], w=[kkb])
                    P.copy("act", kk[:, 0, 1, :], psR[:, 256:384], r=[pRb], a=[kkb])
                    P.copy("act", kk[:, 1, 0, :], psI[:, 384:512], r=[pIb], a=[kkb])
                    P.copy("act", kk[:, 1, 1, :], psI[:, 384:512], r=[pIb], a=[kkb])
                    p1, p1b = p1r.next()
                    p2, p2b = p2r.next()
                    kr2 = kk[:, 0, :, :].rearrange("p b k -> p (b k)")
                    ki2 = kk[:, 1, :, :].rearrange("p b k -> p (b k)")
                    P.tt("dve", p1[:, 0:256], psR[:, 0:256], kr2, ALU.mult, r=[pRb, kkb], w=[p1b])
                    P.tt("dve", p1[:, 256:512], psI[:, 0:256], ki2, ALU.mult, r=[pIb, kkb], a=[p1b])
                    P.tt("dve", p2[:, 0:256], psR[:, 0:256], ki2, ALU.mult, r=[pRb, kkb], w=[p2b])
                    P.tt("dve", p2[:, 256:512], psI[:, 0:256], kr2, ALU.mult, r=[pIb, kkb], a=[p2b])
                    yr_o = Yr[:, k2c, ch, :, :].rearrange("p b k -> p (b k)")
                    yi_o = Yi[:, k2c, ch, :, :].rearrange("p b k -> p (b k)")
                    P.tt("pool", yr_o, p1[:, 0:256], p1[:, 256:512], ALU.subtract, r=[p1b],
                         **({"w": [Yrb]} if firstY else {"a": [Yrb]}))
                    P.tt("pool", yi_o, p2[:, 0:256], p2[:, 256:512], ALU.add, r=[p2b],
                         **({"w": [Yib]} if firstY else {"a": [Yib]}))
                    firstY = False
            firstZ = True
            for ch in range(G):
                for b in range(2):
                    ps, pb = pring.next()
                    i_ = 0
                    for (yt, ytb, rn) in ((Yr, Yrb, "RA"), (Yi, Yib, "RB")):
                        for k2c in range(2):
                            P.mm(ps[:, :], yt[:, k2c, ch, b, :], T[rn][:, k2c, :], i_ == 0, i_ == 3, [ytb, tb], pb)
                            i_ += 1
                    p1, p1b = p1r.next()
                    p2, p2b = p2r.next()
                    P.tt("dve", p1[:, :], ps[:, :], T["UA"][:, :], ALU.mult, r=[pb, tb], w=[p1b])
                    P.tt("dve", p2[:, :], ps[:, :], T["UB"][:, :], ALU.mult, r=[pb, tb], w=[p2b])
                    P.tt("pool", Zr[:, ch, b, :], p1[:, 0:256], p1[:, 256:512], ALU.add, r=[p1b],
                         **({"w": [Zrb]} if firstZ else {"a": [Zrb]}))
                    P.tt("pool", Zi[:, ch, b, :], p2[:, 0:256], p2[:, 256:512], ALU.add, r=[p2b],
                         **({"w": [Zib]} if firstZ else {"a": [Zib]}))
                    firstZ = False
            yt_, ytb_ = yout.next()
            for ch in range(G):
                ps, pb = pring.next()
                P.mm(ps[0:64, :], T["C1i"][:, 0:64], Zr[:, ch, :, :].rearrange("p b n -> p (b n)"), True, False,
                     [tb, Zrb], pb)
                P.mm(ps[0:64, :], T["NS1i"][:, 0:64], Zi[:, ch, :, :].rearrange("p b n -> p (b n)"), False, True,
                     [tb, Zib], pb)
                P.copy("act", yt_[:, ch, :, :].rearrange("p b n -> p (b n)"), ps[0:64, :], r=[pb],
                       **({"w": [ytb_]} if ch == 0 else {"a": [ytb_]}))
            P.dma("sp", yo[:, g * G:(g + 1) * G, :, :], yt_[:, :, :, :], r=[ytb_], on=ytb_, final=True)
        P.finish("sp")
        P.emit()
    return nc


def final_norm(P, c, x, xb, n, h, hb):
    P.act(h[:, :, :n], x[:, :, :n], AF.Square, r=[xb], w=[hb])
    ps, pb = c.aux.next()
    for kc in range(DC):
        P.mm(ps[:, :n], c.ones_d[:, :], h[:, kc, :n], kc == 0, kc == DC - 1, [c.ones_db, hb], pb)
    P.act(c.rs[:, :n], ps[:, :n], AF.Sqrt, r=[pb, c.pvb], w=[c.rsb], bias=pvc(c, "eps"))
    P.recip(c.rs[:, :n], c.rs[:, :n], w=[c.rsb])
    for kc in range(DC):
        P.stt("dve", x[:, kc, :n], x[:, kc, :n], pvc(c, "gfin", kc), c.rs[:, :n], ALU.mult, ALU.mult,
              r=[c.pvb, c.rsb], w=[xb])


def build_C(pvo, npv, kind, with_ctx, final, nblk=8, nb=NB):
    nc = bass.Bass("TRN2", target_bir_lowering=False)
    T = nblk * nb
    nba = nblk + (1 if with_ctx else 0)
    xT = dram_in(nc, "xT", [D, T])
    cc = dram_in(nc, "cc", [128, DC, 2])
    pv_d = dram_in(nc, "pv", [128, npv])
    wmod = dram_in(nc, "wmod", [D, 6 * D])
    wout = dram_in(nc, "wout", [D, D])
    wg = dram_in(nc, "wg", [D, FF])
    wu = dram_in(nc, "wu", [D, FF])
    wd = dram_in(nc, "wd", [FF, D])
    if kind == "hy":
        yT = dram_in(nc, "yT", [D, T])
        vxT = dram_in(nc, "vxT", [D, T])
        x0T = dram_in(nc, "x0T", [D, T])
    else:
        zT = dram_in(nc, "zT", [D, T], BF16)
    xoT = dram_out(nc, "xoT", [D, T])
    if with_ctx:
        cxT = dram_in(nc, "cxT", [D, CTX])
        cyT = dram_in(nc, "cyT", [D, CTX])
        cvxT = dram_in(nc, "cvxT", [D, CTX])
        cx0T = dram_in(nc, "cx0T", [D, CTX])
        cxoT = dram_out(nc, "cxoT", [D, CTX])
    pvobj = PV()
    pvobj.off, pvobj.n = pvo, npv
    with ExitStack() as st:
        P, c = new_prog(nc, st)
        load_pv(P, c, pv_d, pvobj)
        alloc_norm(P, c, nb)
        alloc_ffn(P, c, nb)
        md = adaln(P, c, wmod, cc, "bmod", g2_name="g2")
        x, xb = P.tile("x", [128, DC, nb], F32)
        h, hb = P.tile("h", [128, DC, nb], BF16)
        z, zb = P.tile("z", [128, DC, nb], BF16)
        if kind == "hy":
            y4, y4b = P.tile("y4", [128, 4, nb], F32)
            v4, v4b = P.tile("v4", [128, 4, nb], F32)
            x4, x4b = P.tile("x4", [128, 4, nb], F32)
            gbt, gbb = P.tile("gbt", [128, 2, DC], F32)
            for s in range(2):
                P.tt("dve", gbt[:, s, :], md[s]["gt1"], pvc(c, "bout", 0, DC), ALU.mult, r=[md["buf"], c.pvb],
                     **({"w": [gbb]} if s == 0 else {"a": [gbb]}))
        for k in range(nba):
            isctx = k >= nblk
            s = 1 if isctx else 0
            n = CTX if isctx else nb
            cols = slice(0, CTX) if isctx else slice(k * nb, (k + 1) * nb)
            fmv = lambda ap_: ap_.rearrange("(c p) t -> p c t", p=128)
            P.dma("sp", x[:, :, :n], fmv((cxT if isctx else xT)[:, cols]), w=[xb], on=xb)
            m = md[s]
            if kind == "hy":
                srcs = (cyT, cvxT, cx0T) if isctx else (yT, vxT, x0T)
                for q in range(4):
                    rows = slice(q * 512, (q + 1) * 512)
                    P.dma("sp", y4[:, :, :n], fmv(srcs[0][rows, cols]), w=[y4b], on=y4b)
                    P.dma("sp", v4[:, :, :n], fmv(srcs[1][rows, cols]), w=[v4b], on=v4b)
                    P.dma("sp", x4[:, :, :n], fmv(srcs[2][rows, cols]), w=[x4b], on=x4b)
                    for ci_ in range(4):
                        ch = 4 * q + ci_
                        P.stt("dve", y4[:, ci_, :n], v4[:, ci_, :n], pvc(c, "skip", ch), y4[:, ci_, :n],
                              ALU.mult, ALU.add, r=[v4b, c.pvb], w=[y4b])
                    P.tt("pool", z[:, 4 * q:4 * q + 4, :n], y4[:, :, :n], x4[:, :, :n], ALU.mult, r=[y4b, x4b],
                         **({"w": [zb]} if q == 0 else {"a": [zb]}))
            else:
                P.dma("sp", z[:, :, :n], fmv(zT[:, cols]), w=[zb], on=zb)

            def epi_o(mc, ri, ps, pb, n=n, m=m, s=s):
                P.stt("dve", x[:, mc, :n], ps[:, :n], m["gt1"][:, mc:mc + 1], x[:, mc, :n], ALU.mult, ALU.add,
                      r=[pb, md["buf"]], w=[xb])
                if kind == "hy":
                    P.ts("dve", x[:, mc, :n], x[:, mc, :n], gbt[:, s, mc:mc + 1], None, ALU.add, r=[gbb], w=[xb])
            linear(P, c, wout, D, D, [(z, n, zb)], epi_o)
            norm_mod(P, c, x, xb, n, m["ge2"], m["sh2"], md["geb"], h, hb)
            ffn(P, c, x, xb, h, hb, n, wg, wu, wd, m["gt2"], md["buf"])
            if final and not isctx:
                final_norm(P, c, x, xb, n, h, hb)
            P.dma("sp", fmv((cxoT if isctx else xoT)[:, cols]), x[:, :, :n], r=[xb], on=xb, final=True)
        P.finish("sp")
        P.emit()
    return nc


def build_D(pvo, npv, nblk=8, nb=NB):
    nc = bass.Bass("TRN2", target_bir_lowering=False)
    T = nblk * nb
    nba = nblk + 1
    xT = dram_in(nc, "xT", [D, T])
    xh = dram_in(nc, "xh", [128, nba, DC, 2])
    hmask = dram_in(nc, "hmask", [128, nba, 2])
    cxT = dram_in(nc, "cxT", [D, CTX])
    cc = dram_in(nc, "cc", [128, DC, 2])
    pv_d = dram_in(nc, "pv", [128, npv])
    wmod1 = dram_in(nc, "wmod1", [D, 6 * D])
    wmod2 = dram_in(nc, "wmod2", [D, 6 * D])
    scwin = dram_in(nc, "scwin", [D, 3 * D])
    scwout = dram_in(nc, "scwout", [D, D])
    wg = dram_in(nc, "wg", [D, FF])
    wu = dram_in(nc, "wu", [D, FF])
    wd = dram_in(nc, "wd", [FF, D])
    wqkv = dram_in(nc, "wqkv", [D, 3072])
    cosT = dram_in(nc, "cosT", [128, T])
    sinT = dram_in(nc, "sinT", [128, T])
    pmat = dram_in(nc, "pmat", [128, 128])
    xoT = dram_out(nc, "xoT", [D, T])
    qT = dram_out(nc, "qT", [D, T], BF16)
    kT = dram_out(nc, "kT", [512, T], BF16)
    vT = dram_out(nc, "vT", [512, T], BF16)
    ckT = dram_out(nc, "ckT", [512, CTX], BF16)
    cvT = dram_out(nc, "cvT", [512, CTX], BF16)
    pvobj = PV()
    pvobj.off, pvobj.n = pvo, npv
    with ExitStack() as st:
        P, c = new_prog(nc, st)
        load_pv(P, c, pv_d, pvobj)
        alloc_norm(P, c, nb)
        alloc_ffn(P, c, nb)
        md = adaln(P, c, wmod1, cc, "bmod", g1_name="g1", g2_name="g2", tag="1")
        md2 = adaln(P, c, wmod2, cc, "bmod2", g1_name="g1b", tag="2")
        pm, pmb = P.tile("pm", [128, 128], BF16)
        P.dma("pool", pm[:], pmat[:, :], w=[pmb], on=pmb)
        hm, hmb = P.tile("hm", [128, nba, 2], F32)
        P.dma("sp", hm[:], hmask[:, :, :], w=[hmb], on=hmb)
        x, xb = P.tile("x", [128, DC, nb], F32)
        xht, xhb = P.tile("xht", [128, DC, 2], F32)
        h, hb = P.tile("h", [128, DC, nb], BF16)
        hh, hhb = P.tile("hh", [128, DC, 2], BF16)
        z, zb = P.tile("z", [128, DC, nb], BF16)
        gbs, gbsb = P.tile("gbs", [128, nb], F32)
        gce, gceb = P.tile("gce", [128, nb + 2], F32)
        te, teb = P.tile("te", [128, nb + 2], F32)
        yv, yvb = P.tile("yv", [128, nb], F32)
        cst, cstb = P.tile("cst", [128, nb], F32)
        snt, sntb = P.tile("snt", [128, nb], F32)
        qf, qfb = P.tile("qf", [128, nb], F32)
        sqh, sqhb = P.tile("sqh", [128, nb], BF16)
        rsh, rshb = P.tile("rsh", [128, nb], F32)
        qnr = Ring(P, "qn", 2, [128, nb], BF16)
        t1, t1b = P.tile("t1", [128, nb], F32)
        t2, t2b = P.tile("t2", [128, nb], F32)
        qor = Ring(P, "qo", 3, [128, nb], BF16)
        order = []
        for jx in range(DC):
            order += [jx, DC + jx, 2 * DC + jx]
        fmv = lambda ap_: ap_.rearrange("(c p) t -> p c t", p=128)
        for k in range(nba):
            isctx = k >= nblk
            s = 1 if isctx else 0
            n = CTX if isctx else nb
            cols = slice(0, CTX) if isctx else slice(k * nb, (k + 1) * nb)
            P.dma("sp", x[:, :, :n], fmv((cxT if isctx else xT)[:, cols]), w=[xb], on=xb)
            P.dma("sp", xht[:], xh[:, k, :, :], w=[xhb], on=xhb)
            m = md[s]
            norm_mod(P, c, x, xb, n, m["ge1"], m["sh1"], md["geb"], h, hb)
            norm_mod(P, c, xht, xhb, 2, m["ge1"], m["sh1"], md["geb"], hh, hhb)

            def epi(mc, ri, ps, pb, n=n, k=k):
                kind_, ch = mc // DC, mc % DC
                if kind_ == 0:
                    if ri == 0:
                        P.copy("act", gbs[:, :n], ps[:, :n], r=[pb], w=[gbsb])
                    return
                if kind_ == 1:
                    if ri == 0:
                        P.copy("act", gce[:, 1:n + 1], ps[:, :n], r=[pb], w=[gceb])
                    else:
                        P.ts("dve", gce[:, 0:1], ps[:, 0:1], hm[:, k, 0:1], None, ALU.mult, r=[pb, hmb], a=[gceb])
                        P.ts("dve", gce[:, n + 1:n + 2], ps[:, 1:2], hm[:, k, 1:2], None, ALU.mult, r=[pb, hmb],
                             a=[gceb])
                    return
                if ri == 0:
                    P.tt("dve", te[:, 1:n + 1], ps[:, :n], gce[:, 1:n + 1], ALU.mult, r=[pb, gceb], w=[teb])
                    return
                P.tt("dve", te[:, 0:1], ps[:, 0:1], gce[:, 0:1], ALU.mult, r=[pb, gceb], a=[teb])
                P.tt("dve", te[:, n + 1:n + 2], ps[:, 1:2], gce[:, n + 1:n + 2], ALU.mult, r=[pb, gceb], a=[teb])
                P.act(yv[:, :n], te[:, 1:n + 1], AF.Identity, r=[teb, c.pvb], w=[yvb], scale=pvc(c, "scw1", ch))
                P.stt("dve", yv[:, :n], te[:, 0:n], pvc(c, "scw0", ch), yv[:, :n], ALU.mult, ALU.add,
                      r=[teb, c.pvb], w=[yvb])
                P.stt("dve", yv[:, :n], te[:, 2:n + 2], pvc(c, "scw2", ch), yv[:, :n], ALU.mult, ALU.add,
                      r=[teb, c.pvb], w=[yvb])
                P.tt("pool", z[:, ch, :n], yv[:, :n], gbs[:, :n], ALU.mult, r=[yvb, gbsb],
                     **({"w": [zb]} if ch == 0 else {"a": [zb]}))
            linear(P, c, scwin, D, 3 * D, [(h, n, hb), (hh, 2, hhb)], epi, order=order)

            def epi_o(mc, ri, ps, pb, n=n, m=m):
                P.stt("dve", x[:, mc, :n], ps[:, :n], m["gt1"][:, mc:mc + 1], x[:, mc, :n], ALU.mult, ALU.add,
                      r=[pb, md["buf"]], w=[xb])
            linear(P, c, scwout, D, D, [(z, n, zb)], epi_o)
            norm_mod(P, c, x, xb, n, m["ge2"], m["sh2"], md["geb"], h, hb)
            ffn(P, c, x, xb, h, hb, n, wg, wu, wd, m["gt2"], md["buf"])
            if not isctx:
                P.dma("sp", fmv(xoT[:, cols]), x[:, :, :n], r=[xb], on=xb, final=True)
                P.dma("sp", cst[:, :n], cosT[:, cols], w=[cstb], on=cstb)
                P.dma("sp", snt[:, :n], sinT[:, cols], w=[sntb], on=sntb)
            m2 = md2[s]
            norm_mod(P, c, x, xb, n, m2["ge1"], m2["sh1"], md2["geb"], h, hb)

            def epi_qkv(mc, ri, ps, pb, n=n, isctx=isctx, cols=cols):
                if mc >= 20:
                    o, ob = qor.next()
                    P.copy("act", o[:, :n], ps[:, :n], r=[pb], w=[ob])
                    dst = (cvT if isctx else vT)[(mc - 20) * 128:(mc - 19) * 128, cols]
                    P.dma("sp", dst, o[:, :n], r=[ob], on=ob, final=True)
                    return
                gname = "qg" if mc < 16 else "kg"
                P.copy("act", qf[:, :n], ps[:, :n], r=[pb], w=[qfb])
                P.act(sqh[:, :n], ps[:, :n], AF.Square, r=[pb], w=[sqhb])
                pa, pab = c.aux.next()
                P.mm(pa[:, :n], c.ones_h[:, :], sqh[:, :n], True, True, [c.ones_hb, sqhb], pab)
                P.act(rsh[:, :n], pa[:, :n], AF.Sqrt, r=[pab, c.pvb], w=[rshb], bias=pvc(c, "eps"))
                P.recip(rsh[:, :n], rsh[:, :n], w=[rshb])
                qn, qnb = qnr.next()
                P.stt("dve", qn[:, :n], qf[:, :n], pvc(c, gname), rsh[:, :n], ALU.mult, ALU.mult,
                      r=[qfb, c.pvb, rshb], w=[qnb])
                if isctx:
                    P.dma("sp", ckT[(mc - 16) * 128:(mc - 15) * 128, cols], qn[:, :n], r=[qnb], on=qnb, final=True)
                    return
                pw, pwb = c.aux.next()
                P.mm(pw[:, :n], pm[:, :], qn[:, :n], True, True, [pmb, qnb], pwb)
                P.tt("dve", t1[:, :n], pw[:, :n], snt[:, :n], ALU.mult, r=[pwb, sntb], w=[t1b])
                P.tt("pool", t2[:, :n], qn[:, :n], cst[:, :n], ALU.mult, r=[qnb, cstb], w=[t2b])
                o, ob = qor.next()
                P.tt("pool", o[:, :n], t1[:, :n], t2[:, :n], ALU.add, r=[t1b, t2b], w=[ob])
                dst = qT[mc * 128:(mc + 1) * 128, cols] if mc < 16 else kT[(mc - 16) * 128:(mc - 15) * 128, cols]
                P.dma("sp", dst, o[:, :n], r=[ob], on=ob, final=True)
            if isctx:
                linear(P, c, wqkv, D, 3072, [(h, n, hb)], epi_qkv, order=list(range(16, 24)))
            else:
                linear(P, c, wqkv, D, 3072, [(h, n, hb)], epi_qkv)
        P.finish("sp")
        P.emit()
    return nc


def build_E(T=TC, NK=SEQ + CTX, nb=NB):
    nc = bass.Bass("TRN2", target_bir_lowering=False)
    qT = dram_in(nc, "qT", [D, T], BF16)
    kTa = dram_in(nc, "kTa", [512, NK], BF16)
    Va = dram_in(nc, "Va", [NK, 512], BF16)
    zT = dram_out(nc, "zT", [D, T], BF16)
    NJ = NK // 128
    scale = 1.0 / math.sqrt(HD)
    with ExitStack() as st:
        nc.allow_low_precision("bf16 matmul operands with fp32 accumulation")
        P = Prog(nc, st)
        ones, onesb = P.tile("ones", [128, 128], BF16)
        P.memset("dve", ones[:], 1.0, w=[onesb])
        kh, khb = P.tile("kh", [128, NK], BF16)
        vh, vhb = P.tile("vh", [128, NJ, 128], BF16)
        qr = Ring(P, "q", 2, [128, nb], BF16)
        ptr = Ring(P, "pt", 3, [128, nb], BF16)
        psS = Ring(P, "psS", 3, [128, 512], F32, psum=True)
        psO = Ring(P, "psO", 2, [128, 512], F32, psum=True)
        psD = Ring(P, "psD", 2, [128, 512], F32, psum=True)
        rd, rdb = P.tile("rd", [128, nb], F32)
        zo = Ring(P, "zo", 2, [128, nb], BF16)
        Vv = Va.rearrange("(j p) (h d) -> p j h d", p=128, h=NKV)
        for kvh in range(NKV):
            P.dma("sp", kh[:, :], kTa[kvh * 128:(kvh + 1) * 128, :], w=[khb], on=khb)
            P.dma("sp", vh[:, :, :], Vv[:, :, kvh, :], w=[vhb], on=vhb)
            for g in range(NH // NKV):
                hd_ = kvh * (NH // NKV) + g
                for qb in range(T // nb):
                    q, qb_ = qr.next()
                    cols = slice(qb * nb, (qb + 1) * nb)
                    P.dma("sp", q[:, :], qT[hd_ * 128:(hd_ + 1) * 128, cols], w=[qb_], on=qb_)
                    po, pob = psO.next()
                    pd, pdb = psD.next()
                    sl = []
                    ps, psb = psS.next()
                    P.mm(ps[:, :], kh[:, 0:128], q[:, :], True, True, [khb, qb_], psb)
                    sl.append((ps, psb))
                    for j in range(NJ):
                        if j + 1 < NJ:
                            ps2, psb2 = psS.next()
                            P.mm(ps2[:, :], kh[:, (j + 1) * 128:(j + 2) * 128], q[:, :], True, True, [khb, qb_], psb2)
                            sl.append((ps2, psb2))
                        ps, psb = sl[j]
                        pt, ptb = ptr.next()
                        P.act(pt[:, :], ps[:, :], AF.Exp, r=[psb], w=[ptb], scale=scale)
                        P.mm(po[:, :], vh[:, j, :], pt[:, :], j == 0, j == NJ - 1, [vhb, ptb], pob)
                        P.mm(pd[:, :], ones[:, :], pt[:, :], j == 0, j == NJ - 1, [onesb, ptb], pdb)
                    P.recip(rd[:, :], pd[:, :], r=[pdb], w=[rdb])
                    o, ob = zo.next()
                    P.tt("dve", o[:, :], po[:, :], rd[:, :], ALU.mult, r=[pob, rdb], w=[ob])
                    P.dma("sp", zT[hd_ * 128:(hd_ + 1) * 128, cols], o[:, :], r=[ob], on=ob, final=True)
        P.finish("sp")
        P.emit()
    return nc


CPB = 4


def core_bt(ci):
    return ci // CPB, (ci % CPB) * TC


def pv_C(inp, li, kind, final):
    pv = PV()
    pv.add("eps", np.full((128, 1), EPS, np.float32))
    pv.add_vec("bmod", inp["b_mod"][li])
    pv.add_vec("g2", inp["norm2_g"][li])
    if kind == "hy":
        j = li // 3
        pv.add_vec("skip", inp["hy_skip"][j])
        pv.add_vec("bout", inp["hy_b_out"][j])
    if final:
        pv.add_vec("gfin", inp["final_g"])
    return pv


def pv_D(inp):
    pv = PV()
    pv.add("eps", np.full((128, 1), EPS, np.float32))
    pv.add_vec("bmod", inp["b_mod"][1])
    pv.add_vec("g1", inp["norm1_g"][1])
    pv.add_vec("g2", inp["norm2_g"][1])
    for i in range(3):
        pv.add_vec(f"scw{i}", inp["sc_conv_w"][0][i])
    pv.add_vec("bmod2", inp["b_mod"][2])
    pv.add_vec("g1b", inp["norm1_g"][2])
    pv.add_vec("qg", inp["at_q_g"][0])
    pv.add_vec("kg", inp["at_k_g"][0])
    return pv


def rope_tables(t0, n):
    d_axis = HD // 2
    inv_freq = (np.float32(10000.0) ** (-np.arange(0, d_axis, 2, dtype=np.float32) / np.float32(d_axis))).astype(np.float32)
    t = np.arange(t0, t0 + n)
    rows = (t // 64).astype(np.float32)[:, None]
    cols = (t % 64).astype(np.float32)[:, None]
    ang = np.concatenate([rows * inv_freq[None, :], cols * inv_freq[None, :]], axis=-1).astype(np.float32)
    cos, sin = np.cos(ang), np.sin(ang)
    cosT = np.repeat(cos.T, 2, axis=0)
    sinT = np.repeat(sin.T, 2, axis=0).copy()
    sinT[0::2] *= -1.0
    return np.ascontiguousarray(cosT, dtype=np.float32), np.ascontiguousarray(sinT, dtype=np.float32)


def pair_swap_matrix():
    pm = np.zeros((128, 128), np.float32)
    for m in range(128):
        pm[m ^ 1, m] = 1.0
    return pm


def hyena_layer(inp, li, x_full, xc_full, with_ctx, final):
    j = li // 3
    B = x_full.shape[0]
    pvA, mapsA, npos = maps_A(inp, li, x_full, xc_full, NCORE, TC // NB, NB, SEQ, with_ctx)
    resA = run(build_A(pvA.off, pvA.n, with_ctx, nblk=TC // NB, npos=npos, nb=NB), mapsA)
    vx_full = [np.concatenate([resA[b * CPB + i]["vxT"] for i in range(CPB)], axis=1) for b in range(B)]
    ks_full = np.concatenate([resA[ci]["ksT"] for ci in range(NCORE)], axis=1)
    kd_full = np.concatenate([resA[ci]["kdT"] for ci in range(NCORE)], axis=1)
    tabs = lc_tables()
    CH = D // NCORE
    mapsB = []
    for cj in range(NCORE):
        chs = slice(cj * CH, (cj + 1) * CH)
        cols = np.stack([vx_full[0][chs], vx_full[1][chs], ks_full[chs], kd_full[chs]], axis=1)
        XT = np.ascontiguousarray(cols.reshape(CH, 4, 64, 256).transpose(2, 0, 1, 3))
        mapsB.append(dict(XT=XT, **tabs))
    del vx_full, ks_full, kd_full
    resB = run(build_B(CH=CH, G=8), mapsB)
    del mapsB
    y_full = np.concatenate([resB[cj]["yo"].transpose(2, 1, 0, 3).reshape(2, CH, SEQ) for cj in range(NCORE)], axis=1)
    del resB
    pvC = pv_C(inp, li, "hy", final)
    pca = pvC.array()
    mapsC = []
    for ci in range(NCORE):
        b, t0 = core_bt(ci)
        m = {"xT": np.ascontiguousarray(x_full[b, t0:t0 + TC].T), "cc": cc_arr(inp["c"][b], inp["c_ctx"]), "pv": pca,
             "wmod": inp["w_mod"][li], "wout": inp["hy_w_out"][j], "wg": inp["ffn_w_gate"][li],
             "wu": inp["ffn_w_up"][li], "wd": inp["ffn_w_down"][li],
             "yT": np.ascontiguousarray(y_full[b][:, t0:t0 + TC]), "vxT": resA[ci]["vxT"], "x0T": resA[ci]["x0T"]}
        if with_ctx:
            m.update({"cxT": np.ascontiguousarray(xc_full[b].T), "cyT": resA[ci]["cyT"], "cvxT": resA[ci]["cvxT"],
                      "cx0T": resA[ci]["cx0T"]})
        mapsC.append(m)
    del y_full, resA
    resC = run(build_C(pvC.off, pvC.n, "hy", with_ctx, final, nblk=TC // NB, nb=NB), mapsC)
    del mapsC
    x_new = np.empty_like(x_full)
    for ci in range(NCORE):
        b, t0 = core_bt(ci)
        x_new[b, t0:t0 + TC] = resC[ci]["xoT"].T
    xc_new = None
    if with_ctx:
        xc_new = np.stack([resC[b * CPB]["cxoT"].T for b in range(B)], axis=0)
    return x_new, xc_new


def kernel(**inputs):
    inp = {k: np.asarray(v) for k, v in inputs.items()}
    x_full = np.ascontiguousarray(inp["x"], dtype=np.float32)
    xc_full = np.ascontiguousarray(inp["ctx"], dtype=np.float32)
    B = x_full.shape[0]
    nblk = TC // NB
    x_full, xc_full = hyena_layer(inp, 0, x_full, xc_full, True, False)
    pvD = pv_D(inp)
    pda = pvD.array()
    pm = pair_swap_matrix()
    mapsD = []
    for ci in range(NCORE):
        b, t0 = core_bt(ci)
        xh, hm = halo_arrays(x_full[b], t0, nblk, NB, True)
        cosT, sinT = rope_tables(t0, TC)
        mapsD.append({"xT": np.ascontiguousarray(x_full[b, t0:t0 + TC].T), "xh": xh, "hmask": hm,
                      "cxT": np.ascontiguousarray(xc_full[b].T), "cc": cc_arr(inp["c"][b], inp["c_ctx"]), "pv": pda,
                      "wmod1": inp["w_mod"][1], "wmod2": inp["w_mod"][2], "scwin": inp["sc_w_in"][0],
                      "scwout": inp["sc_w_out"][0], "wg": inp["ffn_w_gate"][1], "wu": inp["ffn_w_up"][1],
                      "wd": inp["ffn_w_down"][1], "wqkv": inp["at_w_qkv"][0], "cosT": cosT, "sinT": sinT, "pmat": pm})
    resD = run(build_D(pvD.off, pvD.n, nblk=nblk, nb=NB), mapsD)
    del mapsD
    for ci in range(NCORE):
        b, t0 = core_bt(ci)
        x_full[b, t0:t0 + TC] = resD[ci]["xoT"].T
    kTa = [np.ascontiguousarray(np.concatenate([resD[b * CPB + i]["kT"] for i in range(CPB)] + [resD[b * CPB]["ckT"]], axis=1))
           for b in range(B)]
    Va = [np.ascontiguousarray(np.concatenate([resD[b * CPB + i]["vT"] for i in range(CPB)] + [resD[b * CPB]["cvT"]], axis=1).T)
          for b in range(B)]
    mapsE = [{"qT": resD[ci]["qT"], "kTa": kTa[ci // CPB], "Va": Va[ci // CPB]} for ci in range(NCORE)]
    resE = run(build_E(), mapsE)
    del mapsE, resD
    pvC = pv_C(inp, 2, "at", False)
    pca = pvC.array()
    mapsC = []
    for ci in range(NCORE):
        b, t0 = core_bt(ci)
        mapsC.append({"xT": np.ascontiguousarray(x_full[b, t0:t0 + TC].T), "cc": cc_arr(inp["c"][b], inp["c_ctx"]),
                      "pv": pca, "wmod": inp["w_mod"][2], "wout": inp["at_w_o"][0], "wg": inp["ffn_w_gate"][2],
                      "wu": inp["ffn_w_up"][2], "wd": inp["ffn_w_down"][2], "zT": resE[ci]["zT"]})
    resC = run(build_C(pvC.off, pvC.n, "at", False, False, nblk=nblk, nb=NB), mapsC)
    del mapsC, resE
    for ci in range(NCORE):
        b, t0 = core_bt(ci)
        x_full[b, t0:t0 + TC] = resC[ci]["xoT"].T
    del resC
    x_full, _ = hyena_layer(inp, 3, x_full, None, False, True)
    return x_full
```
